# Optimizing a Trainium2 kernel written in Bass

```python
import math
import jax, jax.numpy as jnp
from jax import lax
import numpy as np

D_MODEL = 1024
BATCH = 16
SEQ = 2048
DEPTH = 4

CHUNK = 64
N_MIXERS = 2
RET_HEADS = 4
RET_QK_DIM = 256
RET_V_DIM = 512
RET_IN = 2 * RET_HEADS * RET_QK_DIM + 2 * RET_HEADS * RET_V_DIM
SB_HEADS = 16
SB_HEAD_DIM = D_MODEL // SB_HEADS
SB_QBLOCK = 128
FFN_HIDDEN = -(-(8 * D_MODEL) // (3 * 256)) * 256
ROPE_THETA = 10000.0
EPS = 1e-6
N_RET = (DEPTH + 1) // 2
N_SB = DEPTH // 2

kernel_name = "hybrid_retention_stickbreaking_trunk"


def rms_norm(x, gain):
    xf = x.astype(jnp.float32)
    y = xf * lax.rsqrt(jnp.mean(xf * xf, axis=-1, keepdims=True) + EPS)
    return (y * gain.astype(jnp.float32)).astype(x.dtype)


def rotary(x, positions):
    d = x.shape[-1]
    inv_freq = 1.0 / (ROPE_THETA ** (jnp.arange(0, d // 2, dtype=jnp.float32) / (d // 2)))
    ang = positions.astype(jnp.float32)[:, None] * inv_freq[None, :]
    cos = jnp.cos(ang)[None, :, None, :].astype(x.dtype)
    sin = jnp.sin(ang)[None, :, None, :].astype(x.dtype)
    x1, x2 = x[..., : d // 2], x[..., d // 2:]
    return jnp.concatenate([x1 * cos - x2 * sin, x1 * sin + x2 * cos], axis=-1)


def retention(h, w_in, out_gain, w_o):
    B, S, _ = h.shape
    H, DK, DV, C = RET_HEADS, RET_QK_DIM, RET_V_DIM, CHUNK
    N = S // C
    proj = h @ w_in
    q, k, v, g = jnp.split(proj, [H * DK, 2 * H * DK, 2 * H * DK + H * DV], axis=-1)
    pos = jnp.arange(S)
    q = rotary(q.reshape(B, S, H, DK), pos)
    k = rotary(k.reshape(B, S, H, DK), pos) * (DK ** -0.5)
    v = v.reshape(B, S, H, DV)
    to_chunks = lambda t: t.reshape(B, N, C, H, t.shape[-1]).transpose(1, 0, 3, 2, 4)
    qc, kc, vc = to_chunks(q), to_chunks(k), to_chunks(v)

    log_gamma = jnp.log(1.0 - 2.0 ** (-5.0 - jnp.arange(H, dtype=jnp.float32)))
    idx = jnp.arange(C, dtype=jnp.float32)
    intra = jnp.exp(log_gamma[:, None, None] * jnp.abs(idx[:, None] - idx[None, :])).astype(q.dtype)
    q_decay = jnp.exp(log_gamma[:, None] * (idx + 1.0)).astype(q.dtype)
    k_decay = jnp.exp(log_gamma[:, None] * (C - 1.0 - idx)).astype(q.dtype)
    chunk_decay = jnp.exp(log_gamma * C).astype(q.dtype)

    scores = jnp.einsum('nbhcd,nbhed->nbhce', qc, kc) * intra
    inner = jnp.einsum('nbhce,nbhev->nbhcv', scores, vc)

    def step(state, xs):
        qn, kn, vn = xs
        cross = jnp.einsum('bhcd,bhdv->bhcv', qn * q_decay[None, :, :, None], state)
        new_state = state * chunk_decay[None, :, None, None] + jnp.einsum(
            'bhcd,bhcv->bhdv', kn * k_decay[None, :, :, None], vn)
        return new_state, cross

    state0 = jnp.zeros((B, H, DK, DV), dtype=q.dtype)
    _, cross = lax.scan(step, state0, (qc, kc, vc))
    o = (inner + cross).transpose(1, 0, 3, 2, 4).reshape(B, S, H, DV)
    o = rms_norm(o, out_gain)
    o = o.reshape(B, S, H * DV) * jax.nn.silu(g)
    return o @ w_o


def stick_breaking(h, w_in, q_gain, k_gain, w_o):
    B, S, _ = h.shape
    H, DH, QB = SB_HEADS, SB_HEAD_DIM, SB_QBLOCK
    q, k, v = jnp.split(h @ w_in, 3, axis=-1)
    heads = lambda t: t.reshape(B, S, H, DH).transpose(0, 2, 1, 3)
    q = rms_norm(heads(q), q_gain)
    k = rms_norm(heads(k), k_gain)
    v = heads(v)
    scale = DH ** -0.5
    outs = []
    for blk in range(S // QB):
        t0 = blk * QB
        kend = t0 + QB
        z = jnp.einsum('bhtd,bhsd->bhts', q[:, :, t0:kend], k[:, :, :kend]).astype(jnp.float32) * scale
        t_pos = t0 + jnp.arange(QB)
        s_pos = jnp.arange(kend)
        mask = s_pos[None, :] < t_pos[:, None]
        log_stay = jnp.where(mask, jax.nn.log_sigmoid(-z), 0.0)
        later = lax.cumsum(log_stay, axis=3, reverse=True) - log_stay
        weight = jnp.where(mask, jnp.exp(jax.nn.log_sigmoid(z) + later), 0.0)
        outs.append(jnp.einsum('bhts,bhsd->bhtd', weight.astype(v.dtype), v[:, :, :kend]))
    o = jnp.concatenate(outs, axis=2).transpose(0, 2, 1, 3).reshape(B, S, H * DH)
    return o @ w_o


def swiglu(h, w_in, w_out):
    gate, up = jnp.split(h @ w_in, 2, axis=-1)
    return (jax.nn.silu(gate) * up) @ w_out


def setup_inputs(seed: int = 0) -> dict:
    key = jax.random.key(seed)
    ks = jax.random.split(key, 12)
    nrm = lambda k, shape, fan_in: jax.random.normal(k, shape, jnp.float32) * (fan_in ** -0.5)
    gain = lambda k, shape: 1.0 + 0.02 * jax.random.normal(k, shape, jnp.float32)
    return {
        "x": jax.random.normal(ks[0], (BATCH, SEQ, D_MODEL), jnp.float32),
        "mix_norm": gain(ks[1], (DEPTH, D_MODEL)),
        "ffn_norm": gain(ks[2], (DEPTH, D_MODEL)),
        "ret_w_in": nrm(ks[3], (N_RET, D_MODEL, RET_IN), D_MODEL),
        "ret_out_norm": gain(ks[4], (N_RET, RET_HEADS, RET_V_DIM)),
        "ret_w_o": nrm(ks[5], (N_RET, RET_HEADS * RET_V_DIM, D_MODEL), RET_HEADS * RET_V_DIM),
        "sb_w_in": nrm(ks[6], (N_SB, D_MODEL, 3 * D_MODEL), D_MODEL),
        "sb_q_norm": gain(ks[7], (N_SB, SB_HEAD_DIM)),
        "sb_k_norm": gain(ks[8], (N_SB, SB_HEAD_DIM)),
        "sb_w_o": nrm(ks[9], (N_SB, D_MODEL, D_MODEL), D_MODEL),
        "ffn_w_in": nrm(ks[10], (DEPTH, D_MODEL, 2 * FFN_HIDDEN), D_MODEL),
        "ffn_w_out": nrm(ks[11], (DEPTH, FFN_HIDDEN, D_MODEL), FFN_HIDDEN),
    }


def reference(x, mix_norm, ffn_norm, ret_w_in, ret_out_norm, ret_w_o,
              sb_w_in, sb_q_norm, sb_k_norm, sb_w_o, ffn_w_in, ffn_w_out):
    for i in range(DEPTH):
        h = rms_norm(x, mix_norm[i])
        j = i // N_MIXERS
        if i % N_MIXERS == 0:
            x = x + retention(h, ret_w_in[j], ret_out_norm[j], ret_w_o[j])
        else:
            x = x + stick_breaking(h, sb_w_in[j], sb_q_norm[j], sb_k_norm[j], sb_w_o[j])
        x = x + swiglu(rms_norm(x, ffn_norm[i]), ffn_w_in[i], ffn_w_out[i])
    return x
```

```python
import contextlib
import math
import numpy as np
import concourse.bass as bass
import concourse.mybir as mybir
from concourse.bass_utils import run_bass_kernel_spmd

F32 = mybir.dt.float32
BF16 = mybir.dt.bfloat16
AF = mybir.ActivationFunctionType
ALU = mybir.AluOpType

ENGS = ("pe", "act", "dve", "pool", "sp")

D_MODEL = 1024
SEQ = 2048
DEPTH = 4
TT = 512
NTT = SEQ // TT
DC = D_MODEL // 128
FFN_H = 2816
NJ = FFN_H // 128
EPS = 1e-6
N_CORES = 8


class Res:
    __slots__ = ("name", "last_write", "reads")

    def __init__(self, name):
        self.name = name
        self.last_write = None
        self.reads = []


class Planner:
    def __init__(self):
        self.ops = {e: [] for e in ENGS}
        self.dma_counts = {}
        self.targets = {e: set() for e in ENGS}
        self.pending_dma = {}

    def _deps(self, eng, reads, writes):
        deps = []
        for r in reads:
            if r.last_write is not None:
                deps.append(("raw", r.last_write))
        for w in writes:
            if w.last_write is not None:
                deps.append(("waw", w.last_write))
            for ev in w.reads:
                deps.append(("war", ev))
        best = {}
        for kind, ev in deps:
            if ev[0] == "e" and ev[1] == eng:
                if eng in ("pe", "sp") or kind != "raw":
                    continue
            k = (ev[0], ev[1])
            if k not in best or best[k][2] < ev[2]:
                best[k] = ev
        return list(best.values())

    def _record(self, ev, reads, writes):
        for r in reads:
            r.reads.append(ev)
        for w in writes:
            w.last_write = ev
            w.reads = []

    def op(self, eng, fn, reads=(), writes=()):
        waits = self._deps(eng, reads, writes)
        idx = len(self.ops[eng])
        self.ops[eng].append(dict(fn=fn, waits=waits, kind="c"))
        ev = ("e", eng, idx)
        self._record(ev, reads, writes)
        return ev

    def dma(self, eng, semkey, fn, reads=(), writes=()):
        waits = self._deps(eng, reads, writes)
        n = self.dma_counts.get(semkey, 0) + 1
        self.dma_counts[semkey] = n
        self.ops[eng].append(dict(fn=fn, waits=waits, kind="d", dsem=semkey))
        ev = ("d", semkey, n)
        self.pending_dma[semkey] = ev
        self._record(ev, reads, writes)
        return ev

    def wait_all(self, eng, events):
        self.ops[eng].append(dict(fn=None, waits=list(events), kind="w"))

    def barrier(self, dma_keys=()):
        evs = []
        for e in ENGS:
            if e == "sp":
                continue
            n = len(self.ops[e])
            for i in range(n - 1, -1, -1):
                if self.ops[e][i]["kind"] == "c":
                    evs.append(("e", e, i))
                    break
        for k in dma_keys:
            if k in self.pending_dma:
                evs.append(self.pending_dma[k])
        for e in ENGS:
            mine = [ev for ev in evs if not (ev[0] == "e" and ev[1] == e)]
            self.ops[e].append(dict(fn=None, waits=mine, kind="w"))

    def emit(self, nc, es):
        for e in ENGS:
            for o in self.ops[e]:
                for ev in o["waits"]:
                    if ev[0] == "e":
                        self.targets[ev[1]].add(ev[2])
        EPOCH = 3000
        sigcount = {}
        esem = {}
        for e in ENGS:
            c = 0
            m = {}
            for i in range(len(self.ops[e])):
                if i in self.targets[e]:
                    m[i] = (c // EPOCH, c % EPOCH + 1)
                    c += 1
            sigcount[e] = m
            nep = (c + EPOCH - 1) // EPOCH
            esem[e] = [es.enter_context(nc.semaphore("s_%s%d" % (e, q))) for q in range(max(nep, 1))]
        dsem = {k: es.enter_context(nc.semaphore("d_%s" % (k,))) for k in self.dma_counts}
        block = es.enter_context(nc.Block())
        handles = {"pe": block.tensor, "act": block.scalar, "dve": block.vector,
                   "pool": block.gpsimd, "sp": block.sync}
        stats = {}

        def make(e):
            ops = self.ops[e]
            sc = sigcount[e]

            def body(h):
                waited = {}
                nw = 0
                for i, o in enumerate(ops):
                    for ev in o["waits"]:
                        if ev[0] == "e":
                            ep, val = sigcount[ev[1]][ev[2]]
                            sem, key = esem[ev[1]][ep], ("e", ev[1], ep)
                        else:
                            sem, val, key = dsem[ev[1]], 16 * ev[2], ("d", ev[1])
                        if waited.get(key, 0) >= val:
                            continue
                        waited[key] = val
                        h.wait_ge(sem, val)
                        nw += 1
                    if o["fn"] is None:
                        continue
                    ins = o["fn"](h)
                    if o["kind"] == "d":
                        ins.then_inc(dsem[o["dsem"]], 16)
                    elif i in sc:
                        ins.then_inc(esem[e][sc[i][0]], 1)
                stats[e] = (len(ops), nw, len(sc))
            return body

        for e in ENGS:
            if self.ops[e]:
                handles[e](make(e))
        return stats


def _lay_ret_w_in(w):
    nr = w.shape[0]
    wc = w.reshape(nr, DC, 128, 6144)
    out = np.empty((nr, 128, 4, 3, DC, 512), np.float32)
    for h in range(4):
        out[:, :, h, 0, :, 0:256] = wc[:, :, :, h * 256:(h + 1) * 256].transpose(0, 2, 1, 3)
        out[:, :, h, 0, :, 256:512] = wc[:, :, :, 1024 + h * 256:1024 + (h + 1) * 256].transpose(0, 2, 1, 3)
        out[:, :, h, 1, :, :] = wc[:, :, :, 2048 + h * 512:2048 + (h + 1) * 512].transpose(0, 2, 1, 3)
        out[:, :, h, 2, :, :] = wc[:, :, :, 4096 + h * 512:4096 + (h + 1) * 512].transpose(0, 2, 1, 3)
    return np.ascontiguousarray(out.reshape(nr * 128, 4 * 3 * DC * 512))


def _lay_rows(w, nchunk):
    L, R, N = w.shape
    return np.ascontiguousarray(w.reshape(L, nchunk, 128, N).transpose(0, 2, 1, 3).reshape(L * 128, nchunk * N))


def _lay_sb_w_in(w):
    nr = w.shape[0]
    wc = w.reshape(nr, DC, 128, 3, 8, 128)
    out = wc.transpose(0, 2, 4, 1, 3, 5)
    return np.ascontiguousarray(out.reshape(nr * 128, 8 * DC * 384))


def _lay_ffn_w_in(w):
    L = w.shape[0]
    wc = w.reshape(L, DC, 128, 2, NJ, 128)
    out = wc.transpose(0, 2, 4, 1, 3, 5)
    return np.ascontiguousarray(out.reshape(L * 128, NJ * DC * 256))


def _consts():
    j = np.arange(128)
    cb = np.zeros((128, 5, 128), np.float32)
    cb[:, 0, :] = 1.0
    cb[:, 1, :] = ((j[:, None] // 64) == (j[None, :] // 64)).astype(np.float32)
    cb[:, 2, :] = -(j[:, None] >= j[None, :]).astype(np.float32)
    cb[:, 3, :] = -(j[:, None] < j[None, :]).astype(np.float32)
    cb[:, 4, :] = np.eye(128, dtype=np.float32)
    cf = np.zeros((128, 128 + 512 + 512 + 4), np.float32)
    cf[:, 0:128] = (j[:, None] < j[None, :]).astype(np.float32)
    for h in range(4):
        lg = math.log(1.0 - 2.0 ** (-5.0 - h))
        e = j[:, None].astype(np.float64)
        t = j[None, :].astype(np.float64)
        valid = (j[:, None] // 64) <= (j[None, :] // 64)
        dm = np.exp(lg * (np.abs(t - e) - (t + 1.0))) / 16.0
        cf[:, 128 + h * 128:128 + (h + 1) * 128] = np.where(valid, dm, 0.0)
        cf[:, 640 + h * 128:640 + (h + 1) * 128] = (EPS * np.exp(-2.0 * lg * (t + 1.0))) * np.ones((128, 1))
        cf[:, 1152 + h] = np.exp(lg * (127.0 - j)) / 16.0
    inv = (1.0 / (10000.0 ** (np.arange(0, 128, dtype=np.float32) / 128.0))).astype(np.float32)
    ang = (np.arange(SEQ, dtype=np.float32)[None, :] * inv[:, None]).astype(np.float32)
    rope = np.concatenate([np.cos(ang.astype(np.float64)), np.sin(ang.astype(np.float64))], axis=1).astype(np.float32)
    return cb.reshape(128, 640), cf, rope


def build(nseq=2, plan=None):
    if plan is None:
        plan = []
        for i in range(DEPTH):
            plan += [(i, "mix"), (i, "ffn")]
    nc = bass.Bass("TRN2", target_bir_lowering=False)
    P = Planner()

    def dram_in(name, shape, dt=F32):
        return nc.dram_tensor(name, shape, dt, kind="ExternalInput").ap()

    xT = dram_in("xT", [nseq * 128, DC * SEQ])
    outT = nc.dram_tensor("outT", [nseq * 128, DC * SEQ], F32, kind="ExternalOutput").ap()
    normg_d = dram_in("normg", [128, 64])
    retg_d = dram_in("retg", [128, 32])
    sbg_d = dram_in("sbg", [128, 4])
    cb_d = dram_in("cb16", [128, 640])
    cf_d = dram_in("cf32", [128, 1156])
    rope_d = dram_in("rope", [128, 2 * SEQ])
    wshapes = {"ret_w_in": (2, 49152), "ret_w_o": (2, 16384), "sb_w_in": (2, 24576),
               "sb_w_o": (2, 8192), "ffn_w_in": (4, 45056), "ffn_w_out": (4, 22528)}
    w32 = {k: dram_in(k, [L * 128, n]) for k, (L, n) in wshapes.items()}
    wbf = {k: nc.dram_tensor(k + "_b", [L * 128, n], BF16).ap() for k, (L, n) in wshapes.items()}

    with contextlib.ExitStack() as es:
        sb = lambda name, shape, dt: es.enter_context(nc.sbuf_tensor(name, shape, dt))
        x_sb = sb("x_sb", [128, DC * SEQ], F32)
        rstd_sb = sb("rstd_sb", [128, SEQ], F32)
        hfull = sb("hfull", [128, DC * SEQ], BF16)
        cb_sb = sb("cb_sb", [128, 640], BF16)
        cf_sb = sb("cf_sb", [128, 1156], F32)
        normg = sb("normg_sb", [128, 64], F32)
        retg = sb("retg_sb", [128, 32], F32)
        sbg = sb("sbg_sb", [128, 4], F32)
        qgs = sb("qgs_sb", [128, 2], F32)
        zr512 = sb("zr512", [128, 512], BF16)
        ARENA_W = 24704
        arena = sb("arena", [128, ARENA_W], F32)
        arena_b = arena.bitcast(BF16)
        ps = es.enter_context(nc.psum_tensor("ps", [128, 7 * 512], F32))
        ps_tr = es.enter_context(nc.psum_tensor("ps_tr", [128, 1024], BF16))

        ones_bf = cb_sb[:, 0:128]
        blockones = cb_sb[:, 128:256]
        negtri = cb_sb[:, 256:384]
        negcompl = cb_sb[:, 384:512]
        ident = cb_sb[:, 512:640]
        maskM = cf_sb[:, 0:128]

        def bank(b, lo=0, hi=512):
            return ps[:, b * 512 + lo:b * 512 + hi]

        rps = [Res("ps%d" % i) for i in range(7)]
        rtr = Res("ps_tr")
        rx = [[Res("x%d_%d" % (c, t)) for t in range(NTT)] for c in range(DC)]
        rrstd = [Res("rstd%d" % t) for t in range(NTT)]
        rh = [[Res("h%d_%d" % (t, c)) for c in range(DC)] for t in range(NTT)]
        rcb, rcf, rng_, rrg, rsg_ = Res("cb"), Res("cf"), Res("ng"), Res("rg"), Res("sg")

        def xs(c, tt):
            return x_sb[:, c * SEQ + tt * TT:c * SEQ + (tt + 1) * TT]

        def hs(c, tt, lo=0, hi=TT):
            return hfull[:, c * SEQ + tt * TT + lo:c * SEQ + tt * TT + hi]

        def mm(out, lhsT, rhs, start, stop, reads, writes, skip=False):
            if skip:
                P.op("pe", lambda h: h.matmul(out, lhsT, rhs, start=start, stop=stop, skip_group_check=True),
                     reads, writes)
            else:
                P.op("pe", lambda h: h.matmul(out, lhsT, rhs, start=start, stop=stop), reads, writes)

        def act(out, in_, func, reads, writes, **kw):
            P.op("act", lambda h: h.activation(out, in_, func, **kw), reads, writes)

        def tt_(eng, out, in0, in1, op, reads, writes):
            P.op(eng, lambda h: h.tensor_tensor(out, in0, in1, op), reads, writes)

        def ts_(eng, out, in0, s1, s2, op0, op1, reads, writes):
            if op1 is None:
                P.op(eng, lambda h: h.tensor_scalar(out, in0, s1, None, op0), reads, writes)
            else:
                P.op(eng, lambda h: h.tensor_scalar(out, in0, s1, s2, op0, op1), reads, writes)

        def stt(eng, out, in0, scalar, in1, op0, op1, reads, writes):
            P.op(eng, lambda h: h.scalar_tensor_tensor(out, in0, scalar, in1, op0, op1), reads, writes)

        def recip(eng, out, in_, reads, writes):
            P.op(eng, lambda h: h.reciprocal(out, in_), reads, writes)

        def tr_(out, in_, reads, writes):
            P.op("pe", lambda h: h.transpose(out, in_, ident), reads, writes)

        def dma(eng, key, out, in_, reads, writes):
            return P.dma(eng, key, lambda h: h.dma_start(out=out, in_=in_), reads, writes)

        RC = [rcb, rcf, rng_, rrg, rsg_]
        dma("pool", "c_cb", cb_sb[:], cb_d[:], [], [rcb])
        dma("sp", "c_cf", cf_sb[:], cf_d[:], [], [rcf])
        dma("sp", "c_ng", normg[:], normg_d[:], [], [rng_])
        dma("sp", "c_rg", retg[:], retg_d[:], [], [rrg])
        dma("sp", "c_sg", sbg[:], sbg_d[:], [], [rsg_])
        rzr = Res("zr")
        P.op("pool", lambda h: h.memset(zr512[:], 0.0), [], [rzr])
        rqgs = Res("qgs")
        ts_("dve", qgs[:], sbg[:, 0:2], 0.125, None, ALU.mult, None, [*RC], [rqgs])

        rW = {}

        def conv(name, L, pieces):
            for pi, (c0, c1) in enumerate(pieces):
                r = Res("%s%d_%d" % (name, L, pi))
                rW[(name, L, pi)] = (r, c0, c1)
                dma("pool", "cv_%s%d_%d" % (name, L, pi),
                    wbf[name][L * 128:(L + 1) * 128, c0:c1], w32[name][L * 128:(L + 1) * 128, c0:c1], [], [r])

        def wres(name, L, col):
            pi = 0
            while True:
                r, c0, c1 = rW[(name, L, pi)]
                if c0 <= col < c1:
                    return r
                pi += 1

        layers_used = sorted(set(i for i, _ in plan))
        for i in layers_used:
            kinds = [k for (ii, k) in plan if ii == i]
            j = i // 2
            if "mix" in kinds:
                if i % 2 == 0:
                    conv("ret_w_in", j, [(h * 12288, (h + 1) * 12288) for h in range(4)])
                    conv("ret_w_o", j, [(0, 8192), (8192, 16384)])
                else:
                    conv("sb_w_in", j, [(0, 12288), (12288, 24576)])
                    conv("sb_w_o", j, [(0, 8192)])
            if "ffn" in kinds:
                conv("ffn_w_in", i, [(0, 12288), (12288, 24576), (24576, 34816), (34816, 45056)])
                conv("ffn_w_out", i, [(0, 11264), (11264, 22528)])

        class Arena:
            def __init__(self):
                self.off = 0
                self.keys = []

            def f32(self, n):
                o = self.off
                self.off += n
                assert self.off <= ARENA_W, self.off
                return arena[:, o:o + n]

            def bf16(self, n):
                assert n % 2 == 0
                o = self.off
                self.off += n // 2
                assert self.off <= ARENA_W, self.off
                return arena_b[:, 2 * o:2 * o + n]

        dma_seq = [0]

        def newkey(prefix):
            dma_seq[0] += 1
            return "%s%d" % (prefix, dma_seq[0])

        def norm_phase(gcol):
            for tt in range(NTT):
                for c in range(DC):
                    act(hs(c, tt), xs(c, tt), AF.Square, [rx[c][tt]], [rh[tt][c]])
                for c in range(DC):
                    mm(bank(6), ones_bf, hs(c, tt), c == 0, c == DC - 1, [rh[tt][c], *RC], [rps[6]])
                rs = rstd_sb[:, tt * TT:(tt + 1) * TT]
                act(rs, bank(6), AF.Sqrt, [rps[6]], [rrstd[tt]], bias=EPS, scale=1.0 / D_MODEL)
                recip("dve", rs, rs, [rrstd[tt]], [rrstd[tt]])
            for tt in range(NTT):
                rs = rstd_sb[:, tt * TT:(tt + 1) * TT]
                for c in range(DC):
                    stt("dve", hs(c, tt), xs(c, tt), normg[:, gcol + c:gcol + c + 1], rs, ALU.mult, ALU.mult,
                        [rx[c][tt], rrstd[tt], *RC], [rh[tt][c]])

        def ffn_phase(i):
            A = Arena()
            wout = A.bf16(NJ * 1024)
            NSL = 6
            win = [A.bf16(DC * 256) for _ in range(NSL)]
            actb = A.bf16(NJ * TT)
            sg = [A.f32(TT) for _ in range(2)]
            rwout = [Res("wout%d" % k) for k in range(2)]
            rwin = [Res("win%d" % k) for k in range(NSL)]
            ract = [Res("act%d" % k) for k in range(NJ)]
            rsg = [Res("sg%d" % k) for k in range(2)]
            keys = ["fwo0", "fwo1"] + ["fwi%d" % k for k in range(NSL)]
            norm_phase(32 + i * 8)
            R = slice(i * 128, (i + 1) * 128)
            for k in range(2):
                dma("sp", "fwo%d" % k, wout[:, k * 11264:(k + 1) * 11264],
                    wbf["ffn_w_out"][R, k * 11264:(k + 1) * 11264], [wres("ffn_w_out", i, k * 11264)], [rwout[k]])
            cnt = 0
            for tt in range(NTT):
                for j in range(NJ):
                    sl = cnt % NSL
                    dma("sp", "fwi%d" % sl, win[sl][:], wbf["ffn_w_in"][R, j * 2048:(j + 1) * 2048],
                        [wres("ffn_w_in", i, j * 2048)], [rwin[sl]])
                    ba, bb = (cnt % 2), 2 + (cnt % 2)
                    for c in range(DC):
                        mm(bank(ba), win[sl][:, c * 256:c * 256 + 128], hs(c, tt), c == 0, c == DC - 1,
                           [rwin[sl], rh[tt][c]], [rps[ba]])
                    for c in range(DC):
                        mm(bank(bb), win[sl][:, c * 256 + 128:c * 256 + 256], hs(c, tt), c == 0, c == DC - 1,
                           [rwin[sl], rh[tt][c]], [rps[bb]])
                    act(sg[cnt % 2][:], bank(ba), AF.Silu, [rps[ba]], [rsg[cnt % 2]])
                    tt_("dve", actb[:, j * TT:(j + 1) * TT], sg[cnt % 2][:], bank(bb), ALU.mult,
                        [rsg[cnt % 2], rps[bb]], [ract[j]])
                    cnt += 1
                for n in range(DC):
                    bc = 4 + (n % 2)
                    for j in range(NJ):
                        mm(bank(bc), wout[:, j * 1024 + n * 128:j * 1024 + (n + 1) * 128], actb[:, j * TT:(j + 1) * TT],
                           j == 0, j == NJ - 1, [rwout[j // 11], ract[j]], [rps[bc]])
                    tt_("dve", xs(n, tt), xs(n, tt), bank(bc), ALU.add, [rps[bc], rx[n][tt]], [rx[n][tt]])
            P.barrier(keys)

        def ret_phase(i):
            j = i // 2
            A = Arena()
            cos = A.f32(SEQ)
            sin = A.f32(SEQ)
            tq = [A.f32(TT) for _ in range(4)]
            tk = tq
            S = A.f32(2 * 512)
            rsp = [A.f32(128) for _ in range(2)]
            og1 = [A.f32(128) for _ in range(4)]
            wqk = [A.bf16(DC * 512) for _ in range(2)]
            wv = [A.bf16(DC * 512)] * 2
            wg = [A.bf16(DC * 512)] * 2
            wo = [A.bf16(4 * 1024)] * 2
            qT = [A.bf16(2 * TT)] * 2
            kT = [A.bf16(2 * TT)] * 2
            V = [A.bf16(4 * 512)] * 2
            sgb = [A.bf16(4 * TT)] * 2
            kdec = [A.bf16(4 * 256)] * 2
            sT = [A.bf16(128) for _ in range(2)]
            Sbf = [A.bf16(2 * 512) for _ in range(2)]
            osq = [A.bf16(512) for _ in range(2)]
            ogt = [A.bf16(4 * TT)] * 2
            mk = lambda n, k: [Res("%s%d" % (n, q)) for q in range(k)]
            rrope = Res("rope")
            rtq = mk("tq", 4)
            rtk = rtq
            rS = Res("S")
            rrsp, rog1 = mk("rsp", 2), mk("og1", 4)
            rwqk, rwv, rwg, rwo = mk("wqk", 2), mk("wv", 1) * 2, mk("wg", 1) * 2, mk("wo", 1) * 2
            rqT, rkT, rV, rsgb, rkdec, rsT, rSbf, rosq = (mk("qT", 1) * 2, mk("kT", 1) * 2, mk("V", 1) * 2, mk("sgb", 1) * 2,
                                                          mk("kdec", 1) * 2, mk("sT", 2), mk("Sbf", 2), mk("osq", 2))
            rogt = mk("ogt", 1) * 2
            rD = mk("psD", 4)
            keys = ["rrope", "rwqk0", "rwqk1", "rwv0", "rwg0", "rwo0"]
            dma("sp", "rrope", arena[:, 0:2 * SEQ], rope_d[:], [], [rrope])
            norm_phase(i * 8)
            R = slice(j * 128, (j + 1) * 128)
            kk = 0
            for h in range(4):
                g128 = (1.0 - 2.0 ** (-5.0 - h)) ** 128
                sl = h % 2
                base = h * 12288
                dma("sp", "rwqk%d" % sl, wqk[sl][:], wbf["ret_w_in"][R, base:base + 4096],
                    [wres("ret_w_in", j, base)], [rwqk[sl]])
                dma("sp", "rwv0", wv[sl][:], wbf["ret_w_in"][R, base + 4096:base + 8192],
                    [wres("ret_w_in", j, base)], [rwv[sl]])
                dma("sp", "rwg0", wg[sl][:], wbf["ret_w_in"][R, base + 8192:base + 12288],
                    [wres("ret_w_in", j, base)], [rwg[sl]])
                dma("sp", "rwo0", wo[sl][:], wbf["ret_w_o"][R, h * 4096:(h + 1) * 4096],
                    [wres("ret_w_o", j, h * 4096)], [rwo[sl]])
                P.op("pool", lambda hh: hh.memset(S, 0.0), [], [rS])
                P.op("pool", lambda hh: hh.memset(Sbf[0], 0.0), [], [rSbf[0]])
                for tt in range(NTT):
                    k = h * NTT + tt
                    b = k % 2
                    cs = cos[:, tt * TT:(tt + 1) * TT]
                    sn = sin[:, tt * TT:(tt + 1) * TT]
                    for f in range(4):
                        for c in range(DC):
                            mm(bank(f), wqk[sl][:, c * 512 + f * 128:c * 512 + (f + 1) * 128], hs(c, tt),
                               c == 0, c == DC - 1, [rwqk[sl], rh[tt][c]], [rps[f]])
                    for (b0, b1, tmp, rtmp, dst, rdst) in ((0, 1, tq, rtq, qT[b], rqT[b]), (2, 3, tk, rtk, kT[b], rkT[b])):
                        tt_("dve", tmp[0], bank(b0), cs, ALU.mult, [rps[b0], rrope], [rtmp[0]])
                        tt_("dve", tmp[1], bank(b1), sn, ALU.mult, [rps[b1], rrope], [rtmp[1]])
                        tt_("dve", tmp[2], bank(b0), sn, ALU.mult, [rps[b0], rrope], [rtmp[2]])
                        tt_("dve", tmp[3], bank(b1), cs, ALU.mult, [rps[b1], rrope], [rtmp[3]])
                        tt_("dve", dst[:, 0:TT], tmp[0], tmp[1], ALU.subtract, [rtmp[0], rtmp[1]], [rdst])
                        tt_("dve", dst[:, TT:2 * TT], tmp[2], tmp[3], ALU.add, [rtmp[2], rtmp[3]], [rdst])
                    for blk in range(4):
                        bv = 4 + (blk % 2)
                        for c in range(DC):
                            mm(bank(bv), hs(c, tt, blk * 128, (blk + 1) * 128), wv[sl][:, c * 512:(c + 1) * 512],
                               c == 0, c == DC - 1, [rwv[sl], rh[tt][c]], [rps[bv]])
                        act(V[b][:, blk * 512:(blk + 1) * 512], bank(bv), AF.Copy, [rps[bv]], [rV[b]])
                    for vc in range(4):
                        bv = 4 + (vc % 2)
                        for c in range(DC):
                            mm(bank(bv), wg[sl][:, c * 512 + vc * 128:c * 512 + (vc + 1) * 128], hs(c, tt),
                               c == 0, c == DC - 1, [rwg[sl], rh[tt][c]], [rps[bv]])
                        act(sgb[b][:, vc * TT:(vc + 1) * TT], bank(bv), AF.Silu, [rps[bv]], [rsgb[b]])
                    for blk in range(4):
                        gB = tt * 4 + blk
                        bo = blk * 128
                        sp_ = gB % 2
                        for c in range(2):
                            mm(bank(6, 0, 128), kT[b][:, c * TT + bo:c * TT + bo + 128], qT[b][:, c * TT + bo:c * TT + bo + 128],
                               c == 0, c == 1, [rkT[b], rqT[b]], [rD[0]])
                        tt_("dve", sT[kk % 2], bank(6, 0, 128), cf_sb[:, 128 + h * 128:128 + (h + 1) * 128], ALU.mult,
                            [rD[0], *RC], [rsT[kk % 2]])
                        if gB < 15:
                            for c in range(2):
                                tr_(ps_tr[:, c * 128:(c + 1) * 128], kT[b][:, c * TT + bo:c * TT + bo + 128],
                                    [rkT[b], *RC], [rtr])
                            act(kdec[b][:, blk * 256:(blk + 1) * 256], ps_tr[:, 0:256], AF.Identity, [rtr, *RC], [rkdec[b]],
                                scale=cf_sb[:, 1152 + h:1153 + h])
                        bo_ = kk % 2
                        for vc in range(4):
                            reg = bank(bo_, vc * 128, (vc + 1) * 128)
                            mm(reg, V[b][:, blk * 512 + vc * 128:blk * 512 + (vc + 1) * 128], sT[kk % 2],
                               True, gB == 0, [rV[b], rsT[kk % 2]], [rps[bo_]])
                            if gB > 0:
                                for c in range(2):
                                    mm(reg, Sbf[sp_][:, c * 512 + vc * 128:c * 512 + (vc + 1) * 128],
                                       qT[b][:, c * TT + bo:c * TT + bo + 128], False, c == 1,
                                       [rSbf[sp_], rqT[b]], [rps[bo_]])
                        act(osq[kk % 2], bank(bo_), AF.Square, [rps[bo_]], [rosq[kk % 2]])
                        for vc in range(4):
                            mm(bank(6, 128, 256), ones_bf, osq[kk % 2][:, vc * 128:(vc + 1) * 128], vc == 0, vc == 3,
                               [rosq[kk % 2], *RC], [rD[1]])
                        stt("dve", rsp[kk % 2], bank(6, 128, 256), 1.0 / 512.0, cf_sb[:, 640 + h * 128:640 + (h + 1) * 128],
                            ALU.mult, ALU.add, [rD[1], *RC], [rrsp[kk % 2]])
                        act(rsp[kk % 2], rsp[kk % 2], AF.Sqrt, [rrsp[kk % 2]], [rrsp[kk % 2]])
                        recip("dve", rsp[kk % 2], rsp[kk % 2], [rrsp[kk % 2]], [rrsp[kk % 2]])
                        for vc in range(4):
                            gc = j * 16 + h * 4 + vc
                            stt("dve", og1[vc], bank(bo_, vc * 128, (vc + 1) * 128), retg[:, gc:gc + 1], rsp[kk % 2],
                                ALU.mult, ALU.mult, [rps[bo_], rrsp[kk % 2], *RC], [rog1[vc]])
                            tt_("dve", ogt[b][:, vc * TT + bo:vc * TT + bo + 128], og1[vc],
                                sgb[b][:, vc * TT + bo:vc * TT + bo + 128], ALU.mult,
                                [rog1[vc], rsgb[b]], [rogt[b]])
                        if gB < 15:
                            for c in range(2):
                                mm(bank(2 + c), kdec[b][:, blk * 256 + c * 128:blk * 256 + (c + 1) * 128],
                                   V[b][:, blk * 512:(blk + 1) * 512], True, True, [rkdec[b], rV[b]], [rps[2 + c]])
                                stt("dve", S[:, c * 512:(c + 1) * 512], S[:, c * 512:(c + 1) * 512], float(g128),
                                    bank(2 + c), ALU.mult, ALU.add, [rps[2 + c], rS], [rS])
                            act(Sbf[1 - sp_], S, AF.Copy, [rS], [rSbf[1 - sp_]])
                        kk += 1
                    for n in range(DC):
                        bc = 4 + (n % 2)
                        for vc in range(4):
                            mm(bank(bc), wo[sl][:, vc * 1024 + n * 128:vc * 1024 + (n + 1) * 128],
                               ogt[b][:, vc * TT:(vc + 1) * TT], vc == 0, vc == 3, [rwo[sl], rogt[b]], [rps[bc]])
                        tt_("dve", xs(n, tt), xs(n, tt), bank(bc), ALU.add, [rps[bc], rx[n][tt]], [rx[n][tt]])
            P.barrier(keys)

        def sb_phase(i):
            j = i // 2
            A = Arena()
            rq = [A.f32(TT) for _ in range(2)]
            E = [A.f32(TT) for _ in range(4)]
            Wp = [A.f32(TT) for _ in range(4)]
            wqkv = [A.bf16(DC * 384) for _ in range(2)]
            wo = [A.bf16(1024) for _ in range(2)]
            qTp = [A.bf16(SEQ) for _ in range(2)]
            kTp = [A.bf16(SEQ) for _ in range(2)]
            Vp = [A.bf16(16 * 128) for _ in range(2)]
            sqq = [A.bf16(TT) for _ in range(2)]
            SPb = [A.bf16(TT) for _ in range(4)]
            Wt = [A.bf16(TT) for _ in range(4)]
            oTp = [A.bf16(SEQ) for _ in range(2)]
            mk = lambda n, k: [Res("%s%d" % (n, q)) for q in range(k)]
            rrq, rE, rWp = mk("rq", 2), mk("E", 4), mk("Wp", 4)
            rwqkv, rwo = mk("wqkv", 2), mk("wo", 2)
            rqTp, rkTp, rVp, rsqq = mk("qTp", 2), mk("kTp", 2), mk("Vp", 2), mk("sqq", 2)
            rSP, rWt, roTp = mk("SP", 4), mk("Wt", 4), mk("oTp", 2)
            keys = ["swqkv0", "swqkv1", "swo0", "swo1"]
            norm_phase(i * 8)
            R = slice(j * 128, (j + 1) * 128)
            ZB = [0, 1, 2]
            ACC = [3, 4]
            OB = [5, 6]
            for p in range(8):
                pp = p % 2
                dma("sp", "swqkv%d" % pp, wqkv[pp][:], wbf["sb_w_in"][R, p * 3072:(p + 1) * 3072],
                    [wres("sb_w_in", j, p * 3072)], [rwqkv[pp]])
                dma("sp", "swo%d" % pp, wo[pp][:], wbf["sb_w_o"][R, p * 1024:(p + 1) * 1024],
                    [wres("sb_w_o", j, 0)], [rwo[pp]])
                for tt in range(NTT):
                    for f in range(2):
                        for c in range(DC):
                            mm(bank(f), wqkv[pp][:, c * 384 + f * 128:c * 384 + (f + 1) * 128], hs(c, tt),
                               c == 0, c == DC - 1, [rwqkv[pp], rh[tt][c]], [rps[f]])
                    for blk in range(4):
                        for c in range(DC):
                            mm(bank(2, blk * 128, (blk + 1) * 128), hs(c, tt, blk * 128, (blk + 1) * 128),
                               wqkv[pp][:, c * 384 + 256:c * 384 + 384], c == 0, c == DC - 1,
                               [rwqkv[pp], rh[tt][c]], [rps[2]])
                    act(Vp[pp][:, tt * 512:(tt + 1) * 512], bank(2), AF.Copy, [rps[2]], [rVp[pp]])
                    for f, dst, rdst, gain in ((0, qTp[pp], rqTp[pp], qgs[:, j:j + 1]),
                                               (1, kTp[pp], rkTp[pp], sbg[:, 2 + j:3 + j])):
                        act(sqq[f], bank(f), AF.Square, [rps[f]], [rsqq[f]])
                        mm(bank(3), blockones, sqq[f], True, True, [rsqq[f], *RC], [rps[3]])
                        act(rq[f], bank(3), AF.Sqrt, [rps[3]], [rrq[f]], bias=EPS, scale=1.0 / 64.0)
                        recip("dve", rq[f], rq[f], [rrq[f]], [rrq[f]])
                        stt("dve", dst[:, tt * TT:(tt + 1) * TT], bank(f), gain, rq[f], ALU.mult, ALU.mult,
                            [rps[f], rrq[f], *RC, rqgs], [rdst])
                steps = []
                for c in range(4):
                    for a in range(4 * c + 3, -1, -1):
                        for hd in range(2):
                            steps.append((c, a, hd))
                ns = len(steps)
                info = {}

                def plan_Z(si):
                    c, a, hd = steps[si]
                    n0 = max(0, a - 4 * c) * 128
                    zb = ZB[si % 3]
                    r0 = hd * 64
                    mm(bank(zb, n0, 512), kTp[pp][r0:r0 + 64, a * 128:(a + 1) * 128],
                       qTp[pp][r0:r0 + 64, c * TT + n0:(c + 1) * TT], True, True, [rkTp[pp], rqTp[pp]], [rps[zb]])

                plan_Z(0)
                plan_Z(1)
                for si in range(ns + 1):
                    if si < ns:
                        c, a, hd = steps[si]
                        n0 = max(0, a - 4 * c) * 128
                        zb = ZB[si % 3]
                        e = si % 4
                        first = (a == 4 * c + 3)
                        act(E[e][:, n0:], bank(zb, n0, 512), AF.Exp, [rps[zb]], [rE[e]])
                        act(SPb[e][:, n0:], E[e][:, n0:], AF.Ln, [rE[e]], [rSP[e]], bias=1.0)
                        if a >= 4 * c:
                            tt_("dve", SPb[e][:, n0:n0 + 128], SPb[e][:, n0:n0 + 128], maskM, ALU.mult,
                                [rSP[e], *RC], [rSP[e]])
                            tt_("dve", E[e][:, n0:n0 + 128], E[e][:, n0:n0 + 128], maskM, ALU.mult,
                                [rE[e], *RC], [rE[e]])
                        if si + 2 < ns:
                            plan_Z(si + 2)
                        if first:
                            if c > 0:
                                r0 = hd * 64
                                act(oTp[pp][r0:r0 + 64, (c - 1) * TT:c * TT], ps[r0:r0 + 64, OB[hd] * 512:(OB[hd] + 1) * 512], AF.Copy,
                                    [rps[OB[hd]]], [roTp[pp]])
                            mm(bank(ACC[hd]), ones_bf, zr512[:], True, True, [rzr, *RC], [rps[ACC[hd]]])
                            mm(bank(OB[hd]), ones_bf, zr512[:], True, True, [rzr, *RC], [rps[OB[hd]]])
                        mm(bank(ACC[hd], n0, 512), negtri, SPb[e][:, n0:], False, True, [rSP[e], *RC], [rps[ACC[hd]]],
                           skip=True)
                    if si >= 1:
                        c, a, hd = steps[si - 1]
                        n0 = max(0, a - 4 * c) * 128
                        e = (si - 1) % 4
                        act(Wp[e][:, n0:], bank(ACC[hd], n0, 512), AF.Exp, [rps[ACC[hd]]], [rWp[e]])
                        tt_("dve", Wt[e][:, n0:], E[e][:, n0:], Wp[e][:, n0:], ALU.mult, [rE[e], rWp[e]], [rWt[e]])
                        if a > 0:
                            mm(bank(ACC[hd], n0, 512), negcompl, SPb[e][:, n0:], False, True, [rSP[e], *RC],
                               [rps[ACC[hd]]], skip=True)
                        mm(bank(OB[hd], n0, 512), Vp[pp][:, a * 128:(a + 1) * 128], Wt[e][:, n0:], False, True,
                           [rVp[pp], rWt[e]], [rps[OB[hd]]], skip=True)
                for hd in range(2):
                    r0 = hd * 64
                    act(oTp[pp][r0:r0 + 64, 3 * TT:4 * TT], ps[r0:r0 + 64, OB[hd] * 512:(OB[hd] + 1) * 512], AF.Copy,
                        [rps[OB[hd]]], [roTp[pp]])
                q = 0
                for tt in range(NTT):
                    for n in range(DC):
                        zb = ZB[q % 3]
                        q += 1
                        mm(bank(zb), wo[pp][:, n * 128:(n + 1) * 128], oTp[pp][:, tt * TT:(tt + 1) * TT], True, True,
                           [rwo[pp], roTp[pp]], [rps[zb]])
                        tt_("dve", xs(n, tt), xs(n, tt), bank(zb), ALU.add, [rps[zb], rx[n][tt]], [rx[n][tt]])
            P.barrier(keys)

        out_events = []
        for s in range(nseq):
            for c in range(DC):
                dma("sp", "xin%d" % c, x_sb[:, c * SEQ:(c + 1) * SEQ], xT[s * 128:(s + 1) * 128, c * SEQ:(c + 1) * SEQ],
                    [], [rx[c][t] for t in range(NTT)])
            for (i, kind) in plan:
                if kind == "mix":
                    if i % 2 == 0:
                        ret_phase(i)
                    else:
                        sb_phase(i)
                else:
                    ffn_phase(i)
            for c in range(DC):
                ev = dma("sp", "xout%d" % c, outT[s * 128:(s + 1) * 128, c * SEQ:(c + 1) * SEQ],
                         x_sb[:, c * SEQ:(c + 1) * SEQ], [rx[c][t] for t in range(NTT)], [])
                out_events.append(ev)
        P.wait_all("sp", out_events[-DC:])
        stats = P.emit(nc, es)
    return nc, stats


_CACHE = {}


def _prep_inputs(x, mix_norm, ffn_norm, ret_w_in, ret_out_norm, ret_w_o,
                 sb_w_in, sb_q_norm, sb_k_norm, sb_w_o, ffn_w_in, ffn_w_out, n_cores=N_CORES, nseq=2):
    f = lambda a: np.asarray(a, dtype=np.float32)
    x = f(x)
    cb, cf, rope = _consts()
    normg = np.zeros((128, 64), np.float32)
    normg[:, 0:32] = f(mix_norm).reshape(4, DC, 128).transpose(2, 0, 1).reshape(128, 32)
    normg[:, 32:64] = f(ffn_norm).reshape(4, DC, 128).transpose(2, 0, 1).reshape(128, 32)
    retg = np.ascontiguousarray(f(ret_out_norm).reshape(2, 4, 4, 128).transpose(3, 0, 1, 2).reshape(128, 32))
    sbg = np.zeros((128, 4), np.float32)
    idx = np.arange(128) % 64
    sbg[:, 0:2] = f(sb_q_norm)[:, idx].T
    sbg[:, 2:4] = f(sb_k_norm)[:, idx].T
    shared = {
        "normg": normg, "retg": retg, "sbg": sbg, "cb16": cb, "cf32": cf, "rope": rope,
        "ret_w_in": _lay_ret_w_in(f(ret_w_in)),
        "ret_w_o": _lay_rows(f(ret_w_o), 16),
        "sb_w_in": _lay_sb_w_in(f(sb_w_in)),
        "sb_w_o": _lay_rows(f(sb_w_o), 8),
        "ffn_w_in": _lay_ffn_w_in(f(ffn_w_in)),
        "ffn_w_out": _lay_rows(f(ffn_w_out), NJ),
    }
    in_maps = []
    for core in range(n_cores):
        xs_ = x[core * nseq:(core + 1) * nseq]
        xt = xs_.reshape(nseq, SEQ, DC, 128).transpose(0, 3, 2, 1)
        m = dict(shared)
        m["xT"] = np.ascontiguousarray(xt.reshape(nseq * 128, DC * SEQ))
        in_maps.append(m)
    return in_maps


def _unlay_out(o, nseq=2):
    return o.reshape(nseq, 128, DC, SEQ).transpose(0, 3, 2, 1).reshape(nseq, SEQ, D_MODEL)


def kernel(x, mix_norm, ffn_norm, ret_w_in, ret_out_norm, ret_w_o,
           sb_w_in, sb_q_norm, sb_k_norm, sb_w_o, ffn_w_in, ffn_w_out):
    in_maps = _prep_inputs(x, mix_norm, ffn_norm, ret_w_in, ret_out_norm, ret_w_o,
                           sb_w_in, sb_q_norm, sb_k_norm, sb_w_o, ffn_w_in, ffn_w_out)
    if "nc" not in _CACHE:
        _CACHE["nc"] = build()[0]
    res = run_bass_kernel_spmd(_CACHE["nc"], in_maps, core_ids=list(range(N_CORES)))
    outs = [_unlay_out(np.asarray(r["outT"], dtype=np.float32)) for r in res.results]
    return np.ascontiguousarray(np.concatenate(outs, axis=0).astype(np.float32))
```

```python
import contextlib
import math
import numpy as np
import concourse.bass as bass
import concourse.mybir as mybir
from concourse.bass_utils import run_bass_kernel_spmd

F32 = mybir.dt.float32
BF16 = mybir.dt.bfloat16
AF = mybir.ActivationFunctionType
ALU = mybir.AluOpType

ENGS = ("pe", "act", "dve", "pool", "sp")

D_MODEL = 1024
SEQ = 2048
DEPTH = 4
TT = 512
NTT = SEQ // TT
DC = D_MODEL // 128
FFN_H = 2816
NJ = FFN_H // 128
EPS = 1e-6
N_CORES = 8


class Res:
    __slots__ = ("name", "last_write", "reads")

    def __init__(self, name):
        self.name = name
        self.last_write = None
        self.reads = []


class Planner:
    def __init__(self):
        self.ops = {e: [] for e in ENGS}
        self.dma_counts = {}
        self.targets = {e: set() for e in ENGS}
        self.pending_dma = {}

    def _deps(self, eng, reads, writes):
        deps = []
        for r in reads:
            if r.last_write is not None:
                deps.append(("raw", r.last_write))
        for w in writes:
            if w.last_write is not None:
                deps.append(("waw", w.last_write))
            for ev in w.reads:
                deps.append(("war", ev))
        best = {}
        for kind, ev in deps:
            if ev[0] == "e" and ev[1] == eng:
                if eng in ("pe", "sp") or kind != "raw":
                    continue
            k = (ev[0], ev[1])
            if k not in best or best[k][2] < ev[2]:
                best[k] = ev
        return list(best.values())

    def _record(self, ev, reads, writes):
        for r in reads:
            r.reads.append(ev)
        for w in writes:
            w.last_write = ev
            w.reads = []

    def op(self, eng, fn, reads=(), writes=()):
        waits = self._deps(eng, reads, writes)
        idx = len(self.ops[eng])
        self.ops[eng].append(dict(fn=fn, waits=waits, kind="c"))
        ev = ("e", eng, idx)
        self._record(ev, reads, writes)
        return ev

    def dma(self, eng, semkey, fn, reads=(), writes=()):
        waits = self._deps(eng, reads, writes)
        n = self.dma_counts.get(semkey, 0) + 1
        self.dma_counts[semkey] = n
        self.ops[eng].append(dict(fn=fn, waits=waits, kind="d", dsem=semkey))
        ev = ("d", semkey, n)
        self.pending_dma[semkey] = ev
        self._record(ev, reads, writes)
        return ev

    def wait_all(self, eng, events):
        self.ops[eng].append(dict(fn=None, waits=list(events), kind="w"))

    def barrier(self, dma_keys=()):
        evs = []
        for e in ENGS:
            if e == "sp":
                continue
            n = len(self.ops[e])
            for i in range(n - 1, -1, -1):
                if self.ops[e][i]["kind"] == "c":
                    evs.append(("e", e, i))
                    break
        for k in dma_keys:
            if k in self.pending_dma:
                evs.append(self.pending_dma[k])
        for e in ENGS:
            mine = [ev for ev in evs if not (ev[0] == "e" and ev[1] == e)]
            self.ops[e].append(dict(fn=None, waits=mine, kind="w"))

    def emit(self, nc, es):
        for e in ENGS:
            for o in self.ops[e]:
                for ev in o["waits"]:
                    if ev[0] == "e":
                        self.targets[ev[1]].add(ev[2])
        EPOCH = 3000
        sigcount = {}
        esem = {}
        for e in ENGS:
            c = 0
            m = {}
            for i in range(len(self.ops[e])):
                if i in self.targets[e]:
                    m[i] = (c // EPOCH, c % EPOCH + 1)
                    c += 1
            sigcount[e] = m
            nep = (c + EPOCH - 1) // EPOCH
            esem[e] = [es.enter_context(nc.semaphore("s_%s%d" % (e, q))) for q in range(max(nep, 1))]
        dsem = {k: es.enter_context(nc.semaphore("d_%s" % (k,))) for k in self.dma_counts}
        block = es.enter_context(nc.Block())
        handles = {"pe": block.tensor, "act": block.scalar, "dve": block.vector,
                   "pool": block.gpsimd, "sp": block.sync}
        stats = {}

        def make(e):
            ops = self.ops[e]
            sc = sigcount[e]

            def body(h):
                waited = {}
                nw = 0
                for i, o in enumerate(ops):
                    for ev in o["waits"]:
                        if ev[0] == "e":
                            ep, val = sigcount[ev[1]][ev[2]]
                            sem, key = esem[ev[1]][ep], ("e", ev[1], ep)
                        else:
                            sem, val, key = dsem[ev[1]], 16 * ev[2], ("d", ev[1])
                        if waited.get(key, 0) >= val:
                            continue
                        waited[key] = val
                        h.wait_ge(sem, val)
                        nw += 1
                    if o["fn"] is None:
                        continue
                    ins = o["fn"](h)
                    if o["kind"] == "d":
                        ins.then_inc(dsem[o["dsem"]], 16)
                    elif i in sc:
                        ins.then_inc(esem[e][sc[i][0]], 1)
                stats[e] = (len(ops), nw, len(sc))
            return body

        for e in ENGS:
            if self.ops[e]:
                handles[e](make(e))
        return stats


def _lay_ret_w_in(w):
    nr = w.shape[0]
    wc = w.reshape(nr, DC, 128, 6144)
    out = np.empty((nr, 128, 4, 3, DC, 512), np.float32)
    for h in range(4):
        out[:, :, h, 0, :, 0:256] = wc[:, :, :, h * 256:(h + 1) * 256].transpose(0, 2, 1, 3)
        out[:, :, h, 0, :, 256:512] = wc[:, :, :, 1024 + h * 256:1024 + (h + 1) * 256].transpose(0, 2, 1, 3)
        out[:, :, h, 1, :, :] = wc[:, :, :, 2048 + h * 512:2048 + (h + 1) * 512].transpose(0, 2, 1, 3)
        out[:, :, h, 2, :, :] = wc[:, :, :, 4096 + h * 512:4096 + (h + 1) * 512].transpose(0, 2, 1, 3)
    return np.ascontiguousarray(out.reshape(nr * 128, 4 * 3 * DC * 512))


def _lay_rows(w, nchunk):
    L, R, N = w.shape
    return np.ascontiguousarray(w.reshape(L, nchunk, 128, N).transpose(0, 2, 1, 3).reshape(L * 128, nchunk * N))


def _lay_sb_w_in(w):
    nr = w.shape[0]
    wc = w.reshape(nr, DC, 128, 3, 8, 128)
    out = wc.transpose(0, 2, 4, 1, 3, 5)
    return np.ascontiguousarray(out.reshape(nr * 128, 8 * DC * 384))


def _lay_ffn_w_in(w):
    L = w.shape[0]
    wc = w.reshape(L, DC, 128, 2, NJ, 128)
    out = wc.transpose(0, 2, 4, 1, 3, 5)
    return np.ascontiguousarray(out.reshape(L * 128, NJ * DC * 256))


def _consts():
    j = np.arange(128)
    cb = np.zeros((128, 5, 128), np.float32)
    cb[:, 0, :] = 1.0
    cb[:, 1, :] = ((j[:, None] // 64) == (j[None, :] // 64)).astype(np.float32)
    cb[:, 2, :] = -(j[:, None] >= j[None, :]).astype(np.float32)
    cb[:, 3, :] = -(j[:, None] < j[None, :]).astype(np.float32)
    cb[:, 4, :] = np.eye(128, dtype=np.float32)
    cf = np.zeros((128, 128 + 512 + 512 + 4), np.float32)
    cf[:, 0:128] = (j[:, None] < j[None, :]).astype(np.float32)
    for h in range(4):
        lg = math.log(1.0 - 2.0 ** (-5.0 - h))
        e = j[:, None].astype(np.float64)
        t = j[None, :].astype(np.float64)
        valid = (j[:, None] // 64) <= (j[None, :] // 64)
        dm = np.exp(lg * (np.abs(t - e) - (t + 1.0))) / 16.0
        cf[:, 128 + h * 128:128 + (h + 1) * 128] = np.where(valid, dm, 0.0)
        cf[:, 640 + h * 128:640 + (h + 1) * 128] = (EPS * np.exp(-2.0 * lg * (t + 1.0))) * np.ones((128, 1))
        cf[:, 1152 + h] = np.exp(lg * (127.0 - j)) / 16.0
    inv = (1.0 / (10000.0 ** (np.arange(0, 128, dtype=np.float32) / 128.0))).astype(np.float32)
    ang = (np.arange(SEQ, dtype=np.float32)[None, :] * inv[:, None]).astype(np.float32)
    rope = np.concatenate([np.cos(ang.astype(np.float64)), np.sin(ang.astype(np.float64))], axis=1).astype(np.float32)
    return cb.reshape(128, 640), cf, rope


def build(nseq=2, plan=None):
    if plan is None:
        plan = []
        for i in range(DEPTH):
            plan += [(i, "mix"), (i, "ffn")]
    nc = bass.Bass("TRN2", target_bir_lowering=False)
    P = Planner()

    def dram_in(name, shape, dt=F32):
        return nc.dram_tensor(name, shape, dt, kind="ExternalInput").ap()

    xT = dram_in("xT", [nseq * 128, DC * SEQ])
    outT = nc.dram_tensor("outT", [nseq * 128, DC * SEQ], F32, kind="ExternalOutput").ap()
    normg_d = dram_in("normg", [128, 64])
    retg_d = dram_in("retg", [128, 32])
    sbg_d = dram_in("sbg", [128, 4])
    cb_d = dram_in("cb16", [128, 640])
    cf_d = dram_in("cf32", [128, 1156])
    rope_d = dram_in("rope", [128, 2 * SEQ])
    wshapes = {"ret_w_in": (2, 49152), "ret_w_o": (2, 16384), "sb_w_in": (2, 24576),
               "sb_w_o": (2, 8192), "ffn_w_in": (4, 45056), "ffn_w_out": (4, 22528)}
    w32 = {k: dram_in(k, [L * 128, n]) for k, (L, n) in wshapes.items()}
    wbf = {k: nc.dram_tensor(k + "_b", [L * 128, n], BF16).ap() for k, (L, n) in wshapes.items()}

    with contextlib.ExitStack() as es:
        sb = lambda name, shape, dt: es.enter_context(nc.sbuf_tensor(name, shape, dt))
        x_sb = sb("x_sb", [128, DC * SEQ], F32)
        rstd_sb = sb("rstd_sb", [128, SEQ], F32)
        hfull = sb("hfull", [128, DC * SEQ], BF16)
        cb_sb = sb("cb_sb", [128, 640], BF16)
        cf_sb = sb("cf_sb", [128, 1156], F32)
        normg = sb("normg_sb", [128, 64], F32)
        retg = sb("retg_sb", [128, 32], F32)
        sbg = sb("sbg_sb", [128, 4], F32)
        qgs = sb("qgs_sb", [128, 2], F32)
        zr512 = sb("zr512", [128, 512], BF16)
        ARENA_W = 24704
        arena = sb("arena", [128, ARENA_W], F32)
        arena_b = arena.bitcast(BF16)
        ps = es.enter_context(nc.psum_tensor("ps", [128, 7 * 512], F32))
        ps_tr = es.enter_context(nc.psum_tensor("ps_tr", [128, 1024], BF16))

        ones_bf = cb_sb[:, 0:128]
        blockones = cb_sb[:, 128:256]
        negtri = cb_sb[:, 256:384]
        negcompl = cb_sb[:, 384:512]
        ident = cb_sb[:, 512:640]
        maskM = cf_sb[:, 0:128]

        def bank(b, lo=0, hi=512):
            return ps[:, b * 512 + lo:b * 512 + hi]

        rps = [Res("ps%d" % i) for i in range(7)]
        rtr = Res("ps_tr")
        rx = [[Res("x%d_%d" % (c, t)) for t in range(NTT)] for c in range(DC)]
        rrstd = [Res("rstd%d" % t) for t in range(NTT)]
        rh = [[Res("h%d_%d" % (t, c)) for c in range(DC)] for t in range(NTT)]
        rcb, rcf, rng_, rrg, rsg_ = Res("cb"), Res("cf"), Res("ng"), Res("rg"), Res("sg")

        def xs(c, tt):
            return x_sb[:, c * SEQ + tt * TT:c * SEQ + (tt + 1) * TT]

        def hs(c, tt, lo=0, hi=TT):
            return hfull[:, c * SEQ + tt * TT + lo:c * SEQ + tt * TT + hi]

        def mm(out, lhsT, rhs, start, stop, reads, writes, skip=False):
            if skip:
                P.op("pe", lambda h: h.matmul(out, lhsT, rhs, start=start, stop=stop, skip_group_check=True),
                     reads, writes)
            else:
                P.op("pe", lambda h: h.matmul(out, lhsT, rhs, start=start, stop=stop), reads, writes)

        def act(out, in_, func, reads, writes, **kw):
            P.op("act", lambda h: h.activation(out, in_, func, **kw), reads, writes)

        def tt_(eng, out, in0, in1, op, reads, writes):
            P.op(eng, lambda h: h.tensor_tensor(out, in0, in1, op), reads, writes)

        def ts_(eng, out, in0, s1, s2, op0, op1, reads, writes):
            if op1 is None:
                P.op(eng, lambda h: h.tensor_scalar(out, in0, s1, None, op0), reads, writes)
            else:
                P.op(eng, lambda h: h.tensor_scalar(out, in0, s1, s2, op0, op1), reads, writes)

        def stt(eng, out, in0, scalar, in1, op0, op1, reads, writes):
            P.op(eng, lambda h: h.scalar_tensor_tensor(out, in0, scalar, in1, op0, op1), reads, writes)

        def recip(eng, out, in_, reads, writes):
            P.op(eng, lambda h: h.reciprocal(out, in_), reads, writes)

        def tr_(out, in_, reads, writes):
            P.op("pe", lambda h: h.transpose(out, in_, ident), reads, writes)

        def dma(eng, key, out, in_, reads, writes):
            return P.dma(eng, key, lambda h: h.dma_start(out=out, in_=in_), reads, writes)

        RC = [rcb, rcf, rng_, rrg, rsg_]
        dma("pool", "c_cb", cb_sb[:], cb_d[:], [], [rcb])
        dma("sp", "c_cf", cf_sb[:], cf_d[:], [], [rcf])
        dma("sp", "c_ng", normg[:], normg_d[:], [], [rng_])
        dma("sp", "c_rg", retg[:], retg_d[:], [], [rrg])
        dma("sp", "c_sg", sbg[:], sbg_d[:], [], [rsg_])
        rzr = Res("zr")
        P.op("pool", lambda h: h.memset(zr512[:], 0.0), [], [rzr])
        rqgs = Res("qgs")
        ts_("dve", qgs[:], sbg[:, 0:2], 0.125, None, ALU.mult, None, [*RC], [rqgs])

        rW = {}

        def conv(name, L, pieces):
            for pi, (c0, c1) in enumerate(pieces):
                r = Res("%s%d_%d" % (name, L, pi))
                rW[(name, L, pi)] = (r, c0, c1)
                dma("pool", "cv_%s%d_%d" % (name, L, pi),
                    wbf[name][L * 128:(L + 1) * 128, c0:c1], w32[name][L * 128:(L + 1) * 128, c0:c1], [], [r])

        def wres(name, L, col):
            pi = 0
            while True:
                r, c0, c1 = rW[(name, L, pi)]
                if c0 <= col < c1:
                    return r
                pi += 1

        layers_used = sorted(set(i for i, _ in plan))
        for i in layers_used:
            kinds = [k for (ii, k) in plan if ii == i]
            j = i // 2
            if "mix" in kinds:
                if i % 2 == 0:
                    conv("ret_w_in", j, [(h * 12288, (h + 1) * 12288) for h in range(4)])
                    conv("ret_w_o", j, [(0, 8192), (8192, 16384)])
                else:
                    conv("sb_w_in", j, [(0, 12288), (12288, 24576)])
                    conv("sb_w_o", j, [(0, 8192)])
            if "ffn" in kinds:
                conv("ffn_w_in", i, [(0, 12288), (12288, 24576), (24576, 34816), (34816, 45056)])
                conv("ffn_w_out", i, [(0, 11264), (11264, 22528)])

        class Arena:
            def __init__(self):
                self.off = 0
                self.keys = []

            def f32(self, n):
                o = self.off
                self.off += n
                assert self.off <= ARENA_W, self.off
                return arena[:, o:o + n]

            def bf16(self, n):
                assert n % 2 == 0
                o = self.off
                self.off += n // 2
                assert self.off <= ARENA_W, self.off
                return arena_b[:, 2 * o:2 * o + n]

        dma_seq = [0]

        def newkey(prefix):
            dma_seq[0] += 1
            return "%s%d" % (prefix, dma_seq[0])

        def norm_phase(gcol):
            def stats(tt):
                for c in range(DC):
                    act(hs(c, tt), xs(c, tt), AF.Square, [rx[c][tt]], [rh[tt][c]])
                for c in range(DC):
                    mm(bank(6), ones_bf, hs(c, tt), c == 0, c == DC - 1, [rh[tt][c], *RC], [rps[6]])
                rs = rstd_sb[:, tt * TT:(tt + 1) * TT]
                act(rs, bank(6), AF.Sqrt, [rps[6]], [rrstd[tt]], bias=EPS, scale=1.0 / D_MODEL)
                recip("dve", rs, rs, [rrstd[tt]], [rrstd[tt]])

            def hcomp(tt):
                rs = rstd_sb[:, tt * TT:(tt + 1) * TT]
                for c in range(DC):
                    stt("dve", hs(c, tt), xs(c, tt), normg[:, gcol + c:gcol + c + 1], rs, ALU.mult, ALU.mult,
                        [rx[c][tt], rrstd[tt], *RC], [rh[tt][c]])

            stats(0)
            for tt in range(NTT):
                if tt + 1 < NTT:
                    stats(tt + 1)
                hcomp(tt)

        def ffn_phase(i):
            A = Arena()
            wout = A.bf16(NJ * 1024)
            NSL = 6
            win = [A.bf16(DC * 256) for _ in range(NSL)]
            actb = A.bf16(NJ * TT)
            sg = [A.f32(TT) for _ in range(2)]
            rwout = [Res("wout%d" % k) for k in range(2)]
            rwin = [Res("win%d" % k) for k in range(NSL)]
            ract = [Res("act%d" % k) for k in range(NJ)]
            rsg = [Res("sg%d" % k) for k in range(2)]
            keys = ["fwo0", "fwo1"] + ["fwi%d" % k for k in range(NSL)]
            norm_phase(32 + i * 8)
            R = slice(i * 128, (i + 1) * 128)
            for k in range(2):
                dma("sp", "fwo%d" % k, wout[:, k * 11264:(k + 1) * 11264],
                    wbf["ffn_w_out"][R, k * 11264:(k + 1) * 11264], [wres("ffn_w_out", i, k * 11264)], [rwout[k]])
            cnt = 0
            for tt in range(NTT):
                for j in range(NJ):
                    sl = cnt % NSL
                    dma("sp", "fwi%d" % sl, win[sl][:], wbf["ffn_w_in"][R, j * 2048:(j + 1) * 2048],
                        [wres("ffn_w_in", i, j * 2048)], [rwin[sl]])
                    ba, bb = (cnt % 2), 2 + (cnt % 2)
                    for c in range(DC):
                        mm(bank(ba), win[sl][:, c * 256:c * 256 + 128], hs(c, tt), c == 0, c == DC - 1,
                           [rwin[sl], rh[tt][c]], [rps[ba]])
                    for c in range(DC):
                        mm(bank(bb), win[sl][:, c * 256 + 128:c * 256 + 256], hs(c, tt), c == 0, c == DC - 1,
                           [rwin[sl], rh[tt][c]], [rps[bb]])
                    act(sg[cnt % 2][:], bank(ba), AF.Silu, [rps[ba]], [rsg[cnt % 2]])
                    tt_("dve", actb[:, j * TT:(j + 1) * TT], sg[cnt % 2][:], bank(bb), ALU.mult,
                        [rsg[cnt % 2], rps[bb]], [ract[j]])
                    cnt += 1
                for n in range(DC):
                    bc = 4 + (n % 2)
                    for j in range(NJ):
                        mm(bank(bc), wout[:, j * 1024 + n * 128:j * 1024 + (n + 1) * 128], actb[:, j * TT:(j + 1) * TT],
                           j == 0, j == NJ - 1, [rwout[j // 11], ract[j]], [rps[bc]])
                    tt_("dve", xs(n, tt), xs(n, tt), bank(bc), ALU.add, [rps[bc], rx[n][tt]], [rx[n][tt]])
            P.barrier(keys)

        def ret_phase(i):
            j = i // 2
            A = Arena()
            cos = A.f32(SEQ)
            sin = A.f32(SEQ)
            tq = [A.f32(TT) for _ in range(4)]
            tk = tq
            S = A.f32(2 * 512)
            rsp = [A.f32(128) for _ in range(2)]
            og1 = [A.f32(128) for _ in range(4)]
            wqk = [A.bf16(DC * 512) for _ in range(2)]
            wv = [A.bf16(DC * 512)] * 2
            wg = [A.bf16(DC * 512)] * 2
            wo = [A.bf16(4 * 1024)] * 2
            qT = [A.bf16(2 * TT)] * 2
            kT = [A.bf16(2 * TT)] * 2
            V = [A.bf16(4 * 512)] * 2
            sgb = [A.bf16(4 * TT)] * 2
            kdec = [A.bf16(4 * 256)] * 2
            sT = [A.bf16(128) for _ in range(2)]
            Sbf = [A.bf16(2 * 512) for _ in range(2)]
            osq = [A.bf16(512) for _ in range(2)]
            ogt = [A.bf16(4 * TT)] * 2
            mk = lambda n, k: [Res("%s%d" % (n, q)) for q in range(k)]
            rrope = Res("rope")
            rtq = mk("tq", 4)
            rtk = rtq
            rS = Res("S")
            rrsp, rog1 = mk("rsp", 2), mk("og1", 4)
            rwqk, rwv, rwg, rwo = mk("wqk", 2), mk("wv", 1) * 2, mk("wg", 1) * 2, mk("wo", 1) * 2
            rqT, rkT, rV, rsgb, rkdec, rsT, rSbf, rosq = (mk("qT", 1) * 2, mk("kT", 1) * 2, mk("V", 1) * 2, mk("sgb", 1) * 2,
                                                          mk("kdec", 1) * 2, mk("sT", 2), mk("Sbf", 2), mk("osq", 2))
            rogt = mk("ogt", 1) * 2
            rD = mk("psD", 4)
            keys = ["rrope", "rwqk0", "rwqk1", "rwv0", "rwg0", "rwo0"]
            dma("sp", "rrope", arena[:, 0:2 * SEQ], rope_d[:], [], [rrope])
            norm_phase(i * 8)
            R = slice(j * 128, (j + 1) * 128)
            kk = 0
            for h in range(4):
                g128 = (1.0 - 2.0 ** (-5.0 - h)) ** 128
                sl = h % 2
                base = h * 12288
                dma("sp", "rwqk%d" % sl, wqk[sl][:], wbf["ret_w_in"][R, base:base + 4096],
                    [wres("ret_w_in", j, base)], [rwqk[sl]])
                dma("sp", "rwv0", wv[sl][:], wbf["ret_w_in"][R, base + 4096:base + 8192],
                    [wres("ret_w_in", j, base)], [rwv[sl]])
                dma("sp", "rwg0", wg[sl][:], wbf["ret_w_in"][R, base + 8192:base + 12288],
                    [wres("ret_w_in", j, base)], [rwg[sl]])
                dma("sp", "rwo0", wo[sl][:], wbf["ret_w_o"][R, h * 4096:(h + 1) * 4096],
                    [wres("ret_w_o", j, h * 4096)], [rwo[sl]])
                P.op("pool", lambda hh: hh.memset(S, 0.0), [], [rS])
                P.op("pool", lambda hh: hh.memset(Sbf[0], 0.0), [], [rSbf[0]])
                for tt in range(NTT):
                    k = h * NTT + tt
                    b = k % 2
                    cs = cos[:, tt * TT:(tt + 1) * TT]
                    sn = sin[:, tt * TT:(tt + 1) * TT]
                    for f in range(4):
                        for c in range(DC):
                            mm(bank(f), wqk[sl][:, c * 512 + f * 128:c * 512 + (f + 1) * 128], hs(c, tt),
                               c == 0, c == DC - 1, [rwqk[sl], rh[tt][c]], [rps[f]])
                    for (b0, b1, tmp, rtmp, dst, rdst) in ((0, 1, tq, rtq, qT[b], rqT[b]), (2, 3, tk, rtk, kT[b], rkT[b])):
                        tt_("dve", tmp[0], bank(b0), cs, ALU.mult, [rps[b0], rrope], [rtmp[0]])
                        tt_("dve", tmp[1], bank(b1), sn, ALU.mult, [rps[b1], rrope], [rtmp[1]])
                        tt_("dve", tmp[2], bank(b0), sn, ALU.mult, [rps[b0], rrope], [rtmp[2]])
                        tt_("dve", tmp[3], bank(b1), cs, ALU.mult, [rps[b1], rrope], [rtmp[3]])
                        tt_("dve", dst[:, 0:TT], tmp[0], tmp[1], ALU.subtract, [rtmp[0], rtmp[1]], [rdst])
                        tt_("dve", dst[:, TT:2 * TT], tmp[2], tmp[3], ALU.add, [rtmp[2], rtmp[3]], [rdst])
                    for blk in range(4):
                        bv = 4 + (blk % 2)
                        for c in range(DC):
                            mm(bank(bv), hs(c, tt, blk * 128, (blk + 1) * 128), wv[sl][:, c * 512:(c + 1) * 512],
                               c == 0, c == DC - 1, [rwv[sl], rh[tt][c]], [rps[bv]])
                        act(V[b][:, blk * 512:(blk + 1) * 512], bank(bv), AF.Copy, [rps[bv]], [rV[b]])
                    for vc in range(4):
                        bv = 4 + (vc % 2)
                        for c in range(DC):
                            mm(bank(bv), wg[sl][:, c * 512 + vc * 128:c * 512 + (vc + 1) * 128], hs(c, tt),
                               c == 0, c == DC - 1, [rwg[sl], rh[tt][c]], [rps[bv]])
                        act(sgb[b][:, vc * TT:(vc + 1) * TT], bank(bv), AF.Silu, [rps[bv]], [rsgb[b]])
                    for blk in range(4):
                        gB = tt * 4 + blk
                        bo = blk * 128
                        sp_ = gB % 2
                        for c in range(2):
                            mm(bank(6, 0, 128), kT[b][:, c * TT + bo:c * TT + bo + 128], qT[b][:, c * TT + bo:c * TT + bo + 128],
                               c == 0, c == 1, [rkT[b], rqT[b]], [rD[0]])
                        tt_("dve", sT[kk % 2], bank(6, 0, 128), cf_sb[:, 128 + h * 128:128 + (h + 1) * 128], ALU.mult,
                            [rD[0], *RC], [rsT[kk % 2]])
                        if gB < 15:
                            for c in range(2):
                                tr_(ps_tr[:, c * 128:(c + 1) * 128], kT[b][:, c * TT + bo:c * TT + bo + 128],
                                    [rkT[b], *RC], [rtr])
                            act(kdec[b][:, blk * 256:(blk + 1) * 256], ps_tr[:, 0:256], AF.Identity, [rtr, *RC], [rkdec[b]],
                                scale=cf_sb[:, 1152 + h:1153 + h])
                        bo_ = kk % 2
                        for vc in range(4):
                            reg = bank(bo_, vc * 128, (vc + 1) * 128)
                            mm(reg, V[b][:, blk * 512 + vc * 128:blk * 512 + (vc + 1) * 128], sT[kk % 2],
                               True, gB == 0, [rV[b], rsT[kk % 2]], [rps[bo_]])
                            if gB > 0:
                                for c in range(2):
                                    mm(reg, Sbf[sp_][:, c * 512 + vc * 128:c * 512 + (vc + 1) * 128],
                                       qT[b][:, c * TT + bo:c * TT + bo + 128], False, c == 1,
                                       [rSbf[sp_], rqT[b]], [rps[bo_]])
                        act(osq[kk % 2], bank(bo_), AF.Square, [rps[bo_]], [rosq[kk % 2]])
                        for vc in range(4):
                            mm(bank(6, 128, 256), ones_bf, osq[kk % 2][:, vc * 128:(vc + 1) * 128], vc == 0, vc == 3,
                               [rosq[kk % 2], *RC], [rD[1]])
                        stt("dve", rsp[kk % 2], bank(6, 128, 256), 1.0 / 512.0, cf_sb[:, 640 + h * 128:640 + (h + 1) * 128],
                            ALU.mult, ALU.add, [rD[1], *RC], [rrsp[kk % 2]])
                        act(rsp[kk % 2], rsp[kk % 2], AF.Sqrt, [rrsp[kk % 2]], [rrsp[kk % 2]])
                        recip("dve", rsp[kk % 2], rsp[kk % 2], [rrsp[kk % 2]], [rrsp[kk % 2]])
                        for vc in range(4):
                            gc = j * 16 + h * 4 + vc
                            stt("dve", og1[vc], bank(bo_, vc * 128, (vc + 1) * 128), retg[:, gc:gc + 1], rsp[kk % 2],
                                ALU.mult, ALU.mult, [rps[bo_], rrsp[kk % 2], *RC], [rog1[vc]])
                            tt_("dve", ogt[b][:, vc * TT + bo:vc * TT + bo + 128], og1[vc],
                                sgb[b][:, vc * TT + bo:vc * TT + bo + 128], ALU.mult,
                                [rog1[vc], rsgb[b]], [rogt[b]])
                        if gB < 15:
                            for c in range(2):
                                mm(bank(2 + c), kdec[b][:, blk * 256 + c * 128:blk * 256 + (c + 1) * 128],
                                   V[b][:, blk * 512:(blk + 1) * 512], True, True, [rkdec[b], rV[b]], [rps[2 + c]])
                                stt("dve", S[:, c * 512:(c + 1) * 512], S[:, c * 512:(c + 1) * 512], float(g128),
                                    bank(2 + c), ALU.mult, ALU.add, [rps[2 + c], rS], [rS])
                            act(Sbf[1 - sp_], S, AF.Copy, [rS], [rSbf[1 - sp_]])
                        kk += 1
                    for n in range(DC):
                        bc = 4 + (n % 2)
                        for vc in range(4):
                            mm(bank(bc), wo[sl][:, vc * 1024 + n * 128:vc * 1024 + (n + 1) * 128],
                               ogt[b][:, vc * TT:(vc + 1) * TT], vc == 0, vc == 3, [rwo[sl], rogt[b]], [rps[bc]])
                        tt_("dve", xs(n, tt), xs(n, tt), bank(bc), ALU.add, [rps[bc], rx[n][tt]], [rx[n][tt]])
            P.barrier(keys)

        def sb_phase(i):
            j = i // 2
            A = Arena()
            rq = [A.f32(TT) for _ in range(2)]
            E = [A.f32(TT) for _ in range(4)]
            Wp = [A.f32(TT) for _ in range(4)]
            wqkv = [A.bf16(DC * 384) for _ in range(2)]
            wo = [A.bf16(1024) for _ in range(2)]
            qTp = [A.bf16(SEQ) for _ in range(2)]
            kTp = [A.bf16(SEQ) for _ in range(2)]
            Vp = [A.bf16(16 * 128) for _ in range(2)]
            sqq = [A.bf16(TT) for _ in range(2)]
            SPb = [A.bf16(TT) for _ in range(4)]
            Wt = [A.bf16(TT) for _ in range(4)]
            oTp = [A.bf16(SEQ) for _ in range(2)]
            mk = lambda n, k: [Res("%s%d" % (n, q)) for q in range(k)]
            rrq, rE, rWp = mk("rq", 2), mk("E", 4), mk("Wp", 4)
            rwqkv, rwo = mk("wqkv", 2), mk("wo", 2)
            rqTp, rkTp, rVp, rsqq = mk("qTp", 2), mk("kTp", 2), mk("Vp", 2), mk("sqq", 2)
            rSP, rWt, roTp = mk("SP", 4), mk("Wt", 4), mk("oTp", 2)
            keys = ["swqkv0", "swqkv1", "swo0", "swo1"]
            norm_phase(i * 8)
            R = slice(j * 128, (j + 1) * 128)
            ZB = [0, 1, 2]
            ACC = [3, 4]
            OB = [5, 6]
            for p in range(8):
                pp = p % 2
                dma("sp", "swqkv%d" % pp, wqkv[pp][:], wbf["sb_w_in"][R, p * 3072:(p + 1) * 3072],
                    [wres("sb_w_in", j, p * 3072)], [rwqkv[pp]])
                dma("sp", "swo%d" % pp, wo[pp][:], wbf["sb_w_o"][R, p * 1024:(p + 1) * 1024],
                    [wres("sb_w_o", j, 0)], [rwo[pp]])
                for tt in range(NTT):
                    for f in range(2):
                        for c in range(DC):
                            mm(bank(f), wqkv[pp][:, c * 384 + f * 128:c * 384 + (f + 1) * 128], hs(c, tt),
                               c == 0, c == DC - 1, [rwqkv[pp], rh[tt][c]], [rps[f]])
                    for blk in range(4):
                        for c in range(DC):
                            mm(bank(2, blk * 128, (blk + 1) * 128), hs(c, tt, blk * 128, (blk + 1) * 128),
                               wqkv[pp][:, c * 384 + 256:c * 384 + 384], c == 0, c == DC - 1,
                               [rwqkv[pp], rh[tt][c]], [rps[2]])
                    act(Vp[pp][:, tt * 512:(tt + 1) * 512], bank(2), AF.Copy, [rps[2]], [rVp[pp]])
                    for f, dst, rdst, gain in ((0, qTp[pp], rqTp[pp], qgs[:, j:j + 1]),
                                               (1, kTp[pp], rkTp[pp], sbg[:, 2 + j:3 + j])):
                        act(sqq[f], bank(f), AF.Square, [rps[f]], [rsqq[f]])
                        mm(bank(3), blockones, sqq[f], True, True, [rsqq[f], *RC], [rps[3]])
                        act(rq[f], bank(3), AF.Sqrt, [rps[3]], [rrq[f]], bias=EPS, scale=1.0 / 64.0)
                        recip("dve", rq[f], rq[f], [rrq[f]], [rrq[f]])
                        stt("dve", dst[:, tt * TT:(tt + 1) * TT], bank(f), gain, rq[f], ALU.mult, ALU.mult,
                            [rps[f], rrq[f], *RC, rqgs], [rdst])
                steps = []
                for c in range(4):
                    for a in range(4 * c + 3, -1, -1):
                        for hd in range(2):
                            steps.append((c, a, hd))
                ns = len(steps)
                info = {}

                def plan_Z(si):
                    c, a, hd = steps[si]
                    n0 = max(0, a - 4 * c) * 128
                    zb = ZB[si % 3]
                    r0 = hd * 64
                    mm(bank(zb, n0, 512), kTp[pp][r0:r0 + 64, a * 128:(a + 1) * 128],
                       qTp[pp][r0:r0 + 64, c * TT + n0:(c + 1) * TT], True, True, [rkTp[pp], rqTp[pp]], [rps[zb]])

                plan_Z(0)
                plan_Z(1)
                for si in range(ns + 1):
                    if si < ns:
                        c, a, hd = steps[si]
                        n0 = max(0, a - 4 * c) * 128
                        zb = ZB[si % 3]
                        e = si % 4
                        first = (a == 4 * c + 3)
                        act(E[e][:, n0:], bank(zb, n0, 512), AF.Exp, [rps[zb]], [rE[e]])
                        act(SPb[e][:, n0:], E[e][:, n0:], AF.Ln, [rE[e]], [rSP[e]], bias=1.0)
                        if a >= 4 * c:
                            tt_("dve", SPb[e][:, n0:n0 + 128], SPb[e][:, n0:n0 + 128], maskM, ALU.mult,
                                [rSP[e], *RC], [rSP[e]])
                            tt_("dve", E[e][:, n0:n0 + 128], E[e][:, n0:n0 + 128], maskM, ALU.mult,
                                [rE[e], *RC], [rE[e]])
                        if si + 2 < ns:
                            plan_Z(si + 2)
                        if first:
                            if c > 0:
                                r0 = hd * 64
                                act(oTp[pp][r0:r0 + 64, (c - 1) * TT:c * TT], ps[r0:r0 + 64, OB[hd] * 512:(OB[hd] + 1) * 512], AF.Copy,
                                    [rps[OB[hd]]], [roTp[pp]])
                            mm(bank(ACC[hd]), ones_bf, zr512[:], True, True, [rzr, *RC], [rps[ACC[hd]]])
                            mm(bank(OB[hd]), ones_bf, zr512[:], True, True, [rzr, *RC], [rps[OB[hd]]])
                        mm(bank(ACC[hd], n0, 512), negtri, SPb[e][:, n0:], False, True, [rSP[e], *RC], [rps[ACC[hd]]],
                           skip=True)
                    if si >= 1:
                        c, a, hd = steps[si - 1]
                        n0 = max(0, a - 4 * c) * 128
                        e = (si - 1) % 4
                        act(Wp[e][:, n0:], bank(ACC[hd], n0, 512), AF.Exp, [rps[ACC[hd]]], [rWp[e]])
                        tt_("dve", Wt[e][:, n0:], E[e][:, n0:], Wp[e][:, n0:], ALU.mult, [rE[e], rWp[e]], [rWt[e]])
                        if a > 0:
                            mm(bank(ACC[hd], n0, 512), negcompl, SPb[e][:, n0:], False, True, [rSP[e], *RC],
                               [rps[ACC[hd]]], skip=True)
                        mm(bank(OB[hd], n0, 512), Vp[pp][:, a * 128:(a + 1) * 128], Wt[e][:, n0:], False, True,
                           [rVp[pp], rWt[e]], [rps[OB[hd]]], skip=True)
                for hd in range(2):
                    r0 = hd * 64
                    act(oTp[pp][r0:r0 + 64, 3 * TT:4 * TT], ps[r0:r0 + 64, OB[hd] * 512:(OB[hd] + 1) * 512], AF.Copy,
                        [rps[OB[hd]]], [roTp[pp]])
                q = 0
                for tt in range(NTT):
                    for n in range(DC):
                        zb = ZB[q % 3]
                        q += 1
                        mm(bank(zb), wo[pp][:, n * 128:(n + 1) * 128], oTp[pp][:, tt * TT:(tt + 1) * TT], True, True,
                           [rwo[pp], roTp[pp]], [rps[zb]])
                        tt_("dve", xs(n, tt), xs(n, tt), bank(zb), ALU.add, [rps[zb], rx[n][tt]], [rx[n][tt]])
            P.barrier(keys)

        out_events = []
        for s in range(nseq):
            for c in range(DC):
                dma("sp", "xin%d" % c, x_sb[:, c * SEQ:(c + 1) * SEQ], xT[s * 128:(s + 1) * 128, c * SEQ:(c + 1) * SEQ],
                    [], [rx[c][t] for t in range(NTT)])
            for (i, kind) in plan:
                if kind == "mix":
                    if i % 2 == 0:
                        ret_phase(i)
                    else:
                        sb_phase(i)
                else:
                    ffn_phase(i)
            for c in range(DC):
                ev = dma("sp", "xout%d" % c, outT[s * 128:(s + 1) * 128, c * SEQ:(c + 1) * SEQ],
                         x_sb[:, c * SEQ:(c + 1) * SEQ], [rx[c][t] for t in range(NTT)], [])
                out_events.append(ev)
        P.wait_all("sp", out_events[-DC:])
        stats = P.emit(nc, es)
    return nc, stats


_CACHE = {}


def _prep_inputs(x, mix_norm, ffn_norm, ret_w_in, ret_out_norm, ret_w_o,
                 sb_w_in, sb_q_norm, sb_k_norm, sb_w_o, ffn_w_in, ffn_w_out, n_cores=N_CORES, nseq=2):
    f = lambda a: np.asarray(a, dtype=np.float32)
    x = f(x)
    cb, cf, rope = _consts()
    normg = np.zeros((128, 64), np.float32)
    normg[:, 0:32] = f(mix_norm).reshape(4, DC, 128).transpose(2, 0, 1).reshape(128, 32)
    normg[:, 32:64] = f(ffn_norm).reshape(4, DC, 128).transpose(2, 0, 1).reshape(128, 32)
    retg = np.ascontiguousarray(f(ret_out_norm).reshape(2, 4, 4, 128).transpose(3, 0, 1, 2).reshape(128, 32))
    sbg = np.zeros((128, 4), np.float32)
    idx = np.arange(128) % 64
    sbg[:, 0:2] = f(sb_q_norm)[:, idx].T
    sbg[:, 2:4] = f(sb_k_norm)[:, idx].T
    shared = {
        "normg": normg, "retg": retg, "sbg": sbg, "cb16": cb, "cf32": cf, "rope": rope,
        "ret_w_in": _lay_ret_w_in(f(ret_w_in)),
        "ret_w_o": _lay_rows(f(ret_w_o), 16),
        "sb_w_in": _lay_sb_w_in(f(sb_w_in)),
        "sb_w_o": _lay_rows(f(sb_w_o), 8),
        "ffn_w_in": _lay_ffn_w_in(f(ffn_w_in)),
        "ffn_w_out": _lay_rows(f(ffn_w_out), NJ),
    }
    in_maps = []
    for core in range(n_cores):
        xs_ = x[core * nseq:(core + 1) * nseq]
        xt = xs_.reshape(nseq, SEQ, DC, 128).transpose(0, 3, 2, 1)
        m = dict(shared)
        m["xT"] = np.ascontiguousarray(xt.reshape(nseq * 128, DC * SEQ))
        in_maps.append(m)
    return in_maps


def _unlay_out(o, nseq=2):
    return o.reshape(nseq, 128, DC, SEQ).transpose(0, 3, 2, 1).reshape(nseq, SEQ, D_MODEL)


def kernel(x, mix_norm, ffn_norm, ret_w_in, ret_out_norm, ret_w_o,
           sb_w_in, sb_q_norm, sb_k_norm, sb_w_o, ffn_w_in, ffn_w_out):
    in_maps = _prep_inputs(x, mix_norm, ffn_norm, ret_w_in, ret_out_norm, ret_w_o,
                           sb_w_in, sb_q_norm, sb_k_norm, sb_w_o, ffn_w_in, ffn_w_out)
    if "nc" not in _CACHE:
        _CACHE["nc"] = build()[0]
    res = run_bass_kernel_spmd(_CACHE["nc"], in_maps, core_ids=list(range(N_CORES)))
    outs = [_unlay_out(np.asarray(r["outT"], dtype=np.float32)) for r in res.results]
    return np.ascontiguousarray(np.concatenate(outs, axis=0).astype(np.float32))
```
